# Optimizing a Trainium2 kernel written in Bass

```python
import jax, jax.numpy as jnp
from jax import lax
import numpy as np

D_MODEL = 2048
BATCH = 4
SEQ = 8192
DEPTH = 4

N_A_LAYERS = DEPTH // 2
N_B_LAYERS = DEPTH - N_A_LAYERS
GM_WIDTH = 2 * D_MODEL
GM_GROUPS = 16
GM_GROUP_DIM = GM_WIDTH // GM_GROUPS
CHUNK = 128
N_HEADS = 16
QK_NOPE_DIM = 128
QK_ROPE_DIM = 64
V_HEAD_DIM = 128
Q_LORA_RANK = 512
KV_LORA_RANK = 512
ATTN_WIDTH = N_HEADS * V_HEAD_DIM
ROPE_THETA = 10000.0
Q_BLOCK = 128
EPS = 1e-6

kernel_name = "yoco_gmlp_mla_adaln_trunk"


def rms_norm(x, g):
    xf = x.astype(jnp.float32)
    y = xf * lax.rsqrt(jnp.mean(xf * xf, axis=-1, keepdims=True) + EPS)
    return (y * g.astype(jnp.float32)).astype(x.dtype)


def layer_norm(x, g, b):
    xf = x.astype(jnp.float32)
    mu = jnp.mean(xf, axis=-1, keepdims=True)
    xc = xf - mu
    y = xc * lax.rsqrt(jnp.mean(xc * xc, axis=-1, keepdims=True) + EPS)
    return (y * g.astype(jnp.float32) + b.astype(jnp.float32)).astype(x.dtype)


def ada_mod(c, w, b):
    return (jax.nn.silu(c) @ w + b)[:, None, :]


def rope_tables(seq_len):
    pos = jnp.arange(seq_len, dtype=jnp.float32)
    inv_freq = ROPE_THETA ** (-jnp.arange(0, QK_ROPE_DIM, 2, dtype=jnp.float32) / QK_ROPE_DIM)
    ang = pos[:, None] * inv_freq[None, :]
    return jnp.cos(ang), jnp.sin(ang)


def apply_rope(x, cos, sin):
    xf = x.astype(jnp.float32)
    x1, x2 = jnp.split(xf, 2, axis=-1)
    return jnp.concatenate([x1 * cos - x2 * sin, x2 * cos + x1 * sin], axis=-1).astype(x.dtype)


def gmlp_mixer(h, w_in, ln_g, ln_b, w_s, b_s, w_out):
    B, S, _ = h.shape
    u, v, z = jnp.split(h @ w_in, 3, axis=-1)
    u = jax.nn.gelu(u)
    v = layer_norm(jax.nn.gelu(v), ln_g, ln_b)
    causal = jnp.tril(jnp.ones((CHUNK, CHUNK), dtype=bool))
    ws = jnp.where(causal, w_s, 0.0).astype(v.dtype)
    vc = v.reshape(B, S // CHUNK, CHUNK, GM_GROUPS, GM_GROUP_DIM)
    mixed = jnp.einsum('gts,bnsgc->bntgc', ws, vc) + b_s.T[None, None, :, :, None].astype(v.dtype)
    y = u * mixed.reshape(B, S, GM_WIDTH) * jax.nn.silu(z)
    return y @ w_out


def mla_shared_kv(h, w_dkv, g_kva, w_ukv, g_kn, g_kr, cos, sin):
    B, S, _ = h.shape
    c_kv, k_r = jnp.split(h @ w_dkv, [KV_LORA_RANK], axis=-1)
    c_kv = rms_norm(c_kv, g_kva)
    kv = (c_kv @ w_ukv).reshape(B, S, N_HEADS, QK_NOPE_DIM + V_HEAD_DIM)
    k_nope, v = jnp.split(kv, [QK_NOPE_DIM], axis=-1)
    k_nope = rms_norm(k_nope, g_kn)
    k_rope = apply_rope(rms_norm(k_r, g_kr), cos, sin)
    return k_nope, k_rope, v


def causal_block_attention(q_nope, q_rope, k_nope, k_rope, v):
    B, S, H, _ = q_nope.shape
    nb = S // Q_BLOCK
    scale = (QK_NOPE_DIM + QK_ROPE_DIM) ** -0.5
    qn = q_nope.reshape(B, nb, Q_BLOCK, H, QK_NOPE_DIM).transpose(1, 0, 2, 3, 4)
    qr = q_rope.reshape(B, nb, Q_BLOCK, H, QK_ROPE_DIM).transpose(1, 0, 2, 3, 4)
    k_pos = jnp.arange(S)

    def block(args):
        qn_b, qr_b, i = args
        s = (jnp.einsum('bqhd,bkhd->bhqk', qn_b, k_nope)
             + jnp.einsum('bqhr,bkr->bhqk', qr_b, k_rope)).astype(jnp.float32) * scale
        q_pos = i * Q_BLOCK + jnp.arange(Q_BLOCK)
        s = jnp.where(k_pos[None, :] <= q_pos[:, None], s, -jnp.inf)
        p = jax.nn.softmax(s, axis=-1).astype(v.dtype)
        return jnp.einsum('bhqk,bkhd->bqhd', p, v)

    o = lax.map(block, (qn, qr, jnp.arange(nb)))
    return o.transpose(1, 0, 2, 3, 4).reshape(B, S, H, V_HEAD_DIM)


def mla_mixer(h, k_nope, k_rope, v, w_in, g_qa, w_uq, g_qn, g_qr, w_out, cos, sin):
    B, S, _ = h.shape
    c_q, z = jnp.split(h @ w_in, [Q_LORA_RANK], axis=-1)
    q = (rms_norm(c_q, g_qa) @ w_uq).reshape(B, S, N_HEADS, QK_NOPE_DIM + QK_ROPE_DIM)
    q_nope, q_rope = jnp.split(q, [QK_NOPE_DIM], axis=-1)
    q_nope = rms_norm(q_nope, g_qn)
    q_rope = apply_rope(rms_norm(q_rope, g_qr), cos[:, None, :], sin[:, None, :])
    o = causal_block_attention(q_nope, q_rope, k_nope, k_rope, v)
    y = o.reshape(B, S, ATTN_WIDTH) * jax.nn.silu(z)
    return y @ w_out


def setup_inputs(seed: int = 0) -> dict:
    key = jax.random.key(seed)
    ks = iter(jax.random.split(key, 32))
    D = D_MODEL

    def nrm(shape, std):
        return std * jax.random.normal(next(ks), shape, jnp.float32)

    return {
        "x": nrm((BATCH, SEQ, D), 1.0),
        "c": nrm((BATCH, D), 1.0),
        "ada_w": nrm((DEPTH, D, 3 * D), 0.5 * D ** -0.5),
        "ada_b": nrm((DEPTH, 3 * D), 0.02),
        "norm_g": 1.0 + nrm((DEPTH, D), 0.02),
        "a_w_in": nrm((N_A_LAYERS, D, 3 * GM_WIDTH), D ** -0.5),
        "a_ln_g": 1.0 + nrm((N_A_LAYERS, GM_WIDTH), 0.02),
        "a_ln_b": nrm((N_A_LAYERS, GM_WIDTH), 0.02),
        "a_w_s": nrm((N_A_LAYERS, GM_GROUPS, CHUNK, CHUNK), CHUNK ** -0.5),
        "a_b_s": 1.0 + nrm((N_A_LAYERS, GM_GROUPS, CHUNK), 0.1),
        "a_w_out": nrm((N_A_LAYERS, GM_WIDTH, D), GM_WIDTH ** -0.5),
        "kv_ada_w": nrm((D, 2 * D), 0.5 * D ** -0.5),
        "kv_ada_b": nrm((2 * D,), 0.02),
        "kv_norm_g": 1.0 + nrm((D,), 0.02),
        "kv_w_dkv": nrm((D, KV_LORA_RANK + QK_ROPE_DIM), D ** -0.5),
        "kv_g_kva": 1.0 + nrm((KV_LORA_RANK,), 0.02),
        "kv_w_ukv": nrm((KV_LORA_RANK, N_HEADS * (QK_NOPE_DIM + V_HEAD_DIM)), KV_LORA_RANK ** -0.5),
        "kv_g_kn": 1.0 + nrm((QK_NOPE_DIM,), 0.02),
        "kv_g_kr": 1.0 + nrm((QK_ROPE_DIM,), 0.02),
        "b_w_in": nrm((N_B_LAYERS, D, Q_LORA_RANK + ATTN_WIDTH), D ** -0.5),
        "b_g_qa": 1.0 + nrm((N_B_LAYERS, Q_LORA_RANK), 0.02),
        "b_w_uq": nrm((N_B_LAYERS, Q_LORA_RANK, N_HEADS * (QK_NOPE_DIM + QK_ROPE_DIM)), Q_LORA_RANK ** -0.5),
        "b_g_qn": 1.0 + nrm((N_B_LAYERS, QK_NOPE_DIM), 0.02),
        "b_g_qr": 1.0 + nrm((N_B_LAYERS, QK_ROPE_DIM), 0.02),
        "b_w_out": nrm((N_B_LAYERS, ATTN_WIDTH, D), ATTN_WIDTH ** -0.5),
    }


def reference(x, c, ada_w, ada_b, norm_g, a_w_in, a_ln_g, a_ln_b, a_w_s, a_b_s, a_w_out,
              kv_ada_w, kv_ada_b, kv_norm_g, kv_w_dkv, kv_g_kva, kv_w_ukv, kv_g_kn, kv_g_kr,
              b_w_in, b_g_qa, b_w_uq, b_g_qn, b_g_qr, b_w_out):
    cos, sin = rope_tables(x.shape[1])
    k_nope = k_rope = v = None
    for i in range(DEPTH):
        if i == N_A_LAYERS:
            kv_shift, kv_scale = jnp.split(ada_mod(c, kv_ada_w, kv_ada_b), 2, axis=-1)
            h_kv = rms_norm(x, kv_norm_g) * (1.0 + kv_scale) + kv_shift
            k_nope, k_rope, v = mla_shared_kv(h_kv, kv_w_dkv, kv_g_kva, kv_w_ukv, kv_g_kn, kv_g_kr, cos, sin)
        shift, scale, gate = jnp.split(ada_mod(c, ada_w[i], ada_b[i]), 3, axis=-1)
        h = rms_norm(x, norm_g[i]) * (1.0 + scale) + shift
        if i < N_A_LAYERS:
            out = gmlp_mixer(h, a_w_in[i], a_ln_g[i], a_ln_b[i], a_w_s[i], a_b_s[i], a_w_out[i])
        else:
            j = i - N_A_LAYERS
            out = mla_mixer(h, k_nope, k_rope, v, b_w_in[j], b_g_qa[j], b_w_uq[j], b_g_qn[j],
                            b_g_qr[j], b_w_out[j], cos, sin)
        x = x + gate * out
    return x
```

```python
import numpy as np
import ml_dtypes
from contextlib import ExitStack
import concourse.bass as bass
import concourse.mybir as mybir
from concourse.bass_utils import run_bass_kernel_spmd

F32 = mybir.dt.float32
BF16 = mybir.dt.bfloat16
AF = mybir.ActivationFunctionType
ALU = mybir.AluOpType
AX = mybir.AxisListType

D = 2048
NBLK = 32
NTILE = 8
TOK = 4096
EPS = 1e-6
ENGS = ("pe", "act", "dve", "pool", "sp")
SM_SCALE = float(192 ** -0.5)
NEG = -30000.0


class Op:
    __slots__ = ("eng", "fn", "deps", "dma", "sig", "semid", "val", "inc")


class Prog:
    LIMIT = 30000
    DLIM = 1800

    def __init__(self):
        self.ops = {e: [] for e in ENGS}
        self.all = []
        self.lastw = {}
        self.readers = {}
        self.last_on = {}
        self.dmas_since = []
        self.pending = {}

    def add(self, eng, fn, r=(), w=(), dma=None, inc=16):
        op = Op()
        op.eng, op.fn, op.dma, op.sig, op.semid, op.val = eng, fn, dma, False, None, 0
        op.inc = inc
        deps = {}
        for k in r:
            x = self.lastw.get(k)
            if x is not None:
                deps[x] = True
        for k in w:
            x = self.lastw.get(k)
            if x is not None:
                deps.setdefault(x, False)
            for x in self.readers.get(k, ()):
                deps.setdefault(x, False)
        p = self.pending.pop(eng, None)
        if p:
            for x in p:
                deps[x] = True
        op.deps = deps
        for k in r:
            self.readers.setdefault(k, []).append(op)
        for k in w:
            self.lastw[k] = op
            self.readers[k] = []
        self.ops[eng].append(op)
        self.all.append(op)
        self.last_on[eng] = op
        if dma is not None:
            self.dmas_since.append(op)
        return op

    def barrier(self):
        deps = set(self.last_on.values()) | set(self.dmas_since)
        for e in ENGS:
            self.pending.setdefault(e, set()).update(deps)
        self.dmas_since = []

    def finalize(self, nc, stack):
        for op in self.all:
            for d, raw in op.deps.items():
                if d.eng == op.eng and d.dma is None and op.dma is None and (op.eng == "pe"):
                    continue
                d.sig = True
        cnt = {e: 0 for e in ENGS}
        dcnt = {}
        final = {}
        for op in self.all:
            if op.dma is not None:
                c = dcnt.get(op.dma, 0) + 1
                dcnt[op.dma] = c
                ep = (c - 1) // self.DLIM
                op.semid = ("d", op.dma, ep)
                op.val = op.inc * (c - ep * self.DLIM)
                final[op.semid] = op.val
            elif op.sig:
                c = cnt[op.eng] + 1
                cnt[op.eng] = c
                ep = (c - 1) // self.LIMIT
                op.semid = ("e", op.eng, ep)
                op.val = c - ep * self.LIMIT
                final[op.semid] = op.val
        sems = {}
        for i, sid in enumerate(final):
            sems[sid] = stack.enter_context(nc.semaphore("s%d" % i))
        self.sems, self.final = sems, final
        block = stack.enter_context(nc.Block())

        def emit(e, eng):
            seen = {}
            for op in self.ops[eng]:
                need = {}
                for d, raw in op.deps.items():
                    if d is op:
                        continue
                    if d.eng == eng and d.dma is None and op.dma is None and (eng == "pe"):
                        continue
                    if need.get(d.semid, 0) < d.val:
                        need[d.semid] = d.val
                for s, v in need.items():
                    if seen.get(s, 0) >= v:
                        continue
                    e.wait_ge(sems[s], v)
                    seen[s] = v
                ins = op.fn(e)
                if op.dma is not None:
                    ins.then_inc(sems[op.semid], op.inc)
                elif op.sig:
                    ins.then_inc(sems[op.semid], 1)
            if eng == "sp":
                for s, v in final.items():
                    if s[0] == "d":
                        e.wait_ge(sems[s], v)

        @block.tensor
        def _(e):
            emit(e, "pe")

        @block.scalar
        def _(e):
            emit(e, "act")

        @block.vector
        def _(e):
            emit(e, "dve")

        @block.gpsimd
        def _(e):
            emit(e, "pool")

        @block.sync
        def _(e):
            emit(e, "sp")


class Arena:
    def __init__(self, ap):
        self.ap = ap
        self.off = 0
        self.hi = 0
        self.log = []

    def mark(self):
        return self.off

    def reset(self, m):
        self.off = m

    def get(self, shape, dt, parts=128):
        n = 1
        for s in shape:
            n *= s
        nb = n * (4 if dt == F32 else 2)
        nb = (nb + 63) // 64 * 64
        o = self.off
        self.log.append((o, tuple(shape), dt == F32, parts))
        self.off += nb
        self.hi = max(self.hi, self.off)
        a = self.ap[0:parts, o // 2:(o + nb) // 2]
        if dt == F32:
            a = a.bitcast(F32)
        a = a[:, 0:n]
        if len(shape) == 2:
            a = a.rearrange("p (a b) -> p a b", a=shape[0])
        elif len(shape) == 3:
            a = a.rearrange("p (a b c) -> p a b c", a=shape[0], b=shape[1])
        return a


def build(mode="AB", ntiles=NTILE, dbg=None):
    nc = bass.Bass("TRN2", target_bir_lowering=False)
    P = Prog()
    doA = "A" in mode
    doB = "B" in mode
    fused = doA and doB

    def din(name, shape, dt=F32):
        return nc.dram_tensor(name, list(shape), dt, kind="ExternalInput").ap()

    def dout(name, shape, dt=F32):
        return nc.dram_tensor(name, list(shape), dt, kind="ExternalOutput").ap()

    def dint(name, shape, dt=F32):
        return nc.dram_tensor(name, list(shape), dt).ap()

    TOKA = 2 * TOK if fused else TOK
    x_in = din("x", [TOKA, D])
    c_pp = din("c_pp", [128, 16])
    consts = din("consts", [128, 384])
    ada_w = din("ada_w", [4, D, 3 * D])
    ada_b = din("ada_b", [4, 3 * D])
    norm_g = din("norm_g", [4, D])
    if doA:
        a_w_in = din("a_w_in", [2, D, 12288])
        a_ln_g = din("a_ln_g", [2, 4096])
        a_ln_b = din("a_ln_b", [2, 4096])
        a_w_s = din("a_w_s", [2, 16, 128, 128])
        a_b_s = din("a_b_s", [2, 16, 128])
        a_w_out = din("a_w_out", [2, 4096, D])
        kv_ada_w = din("kv_ada_w", [D, 2 * D])
        kv_ada_b = din("kv_ada_b", [1, 2 * D])
        kv_norm_g = din("kv_norm_g", [1, D])
        kv_w_dkv = din("kv_w_dkv", [D, 576])
        kv_g_kva = din("kv_g_kva", [128, 4])
        kv_w_ukv = din("kv_w_ukv", [512, 4096])
        kv_g_kn = din("kv_g_kn", [1, 128])
        kv_g_kr = din("kv_g_kr", [1, 64])
    NBA = TOKA // 128
    rope = din("rope", [128, 2, NBA, 32])
    if doB:
        b_w_in = din("b_w_in", [2, D, 2560])
        b_g_qa = din("b_g_qa", [2, 128, 4])
        b_w_uq = din("b_w_uq", [2, 512, 3072])
        b_g_qn = din("b_g_qn", [2, 128])
        b_g_qr = din("b_g_qr", [2, 64])
        b_w_out = din("b_w_out", [2, D, D])
        amask = din("amask", [128, 8, 512], BF16)
    y_out = None
    if fused:
        kT_g = dint("kT_g", [2 * 16 * 128, TOK], BF16)
        krT_g = dint("krT_g", [2 * 64, TOK], BF16)
        v_g = dint("v_g", [2 * 16 * 128, NBLK * 128], BF16)
        xA = dint("xA", [TOKA, D])
        xB = dint("xB", [TOKA, D])
        y_out = dout("y", [TOK, D])
    elif doA:
        kT_loc = dout("kT_loc", [16 * 128, TOK], BF16)
        krT_loc = dout("krT_loc", [64, TOK], BF16)
        v_loc = dout("v_loc", [16 * 128, NBLK * 128], BF16)
        xA = dout("xA", [TOK, D]) if (dbg and dbg.get("xa_out")) else dint("xA", [TOK, D])
        xB = dout("xB", [TOK, D])
    else:
        kT_g = din("kT_g", [2 * 16 * 128, TOK], BF16)
        krT_g = din("krT_g", [2 * 64, TOK], BF16)
        v_g = din("v_g", [2 * 16 * 128, NBLK * 128], BF16)
        xB = x_in
        xA = dint("xA", [TOK, D])
        y_out = dout("y", [TOK, D])

    stack = ExitStack()
    with stack:
        arena_t = stack.enter_context(nc.sbuf_tensor("arena", [128, 103 * 1024], BF16))
        AR = Arena(arena_t[:])
        psf = [stack.enter_context(nc.psum_tensor("ps%d" % i, [128, 512], F32)) for i in range(8)]
        PSF = [p[:] for p in psf]
        PSB = [p[:].bitcast(BF16) for p in psf]

        identf = AR.get([128], F32)
        trif = AR.get([128], F32)
        identb = AR.get([128], BF16)
        onesb = AR.get([128], BF16)
        cpp = AR.get([16], F32)
        scpp = AR.get([16], BF16)
        screp = AR.get([16, 128], BF16)
        App = AR.get([16], F32)
        Bpp = AR.get([16], F32)
        gate_bc = AR.get([D], F32)
        wbuf = [AR.get([16, 512], BF16) for _ in range(3)]
        xio = [AR.get([D], F32) for _ in range(4)]
        hT = AR.get([16, 512], BF16)
        xsb = [AR.get([D], BF16) for _ in range(2)]
        tmpS, tmpB = xio[0], xio[1]
        KS, KB = ("xio", 0), ("xio", 1)
        rtmp = [xsb[0][:, 0:1024].bitcast(F32), xsb[0][:, 1024:2048].bitcast(F32)]
        J = {"junk": None, "keys": [], "rope": None}
        ss4 = AR.get([4], F32)
        rs4 = AR.get([4], F32)
        persist_mark = AR.mark()

        st = {"wb": 0, "xio": 0, "xsb": 0, "rt": 0}

        def next_wb():
            i = st["wb"] % 3
            st["wb"] += 1
            return i

        def next_xio():
            i = st["xio"] % 4
            st["xio"] += 1
            return i

        ps_pool = {"n": 0, "banks": list(range(8))}

        def next_ps():
            b = ps_pool["banks"][ps_pool["n"] % len(ps_pool["banks"])]
            ps_pool["n"] += 1
            return b

        def set_ps_banks(banks):
            ps_pool["banks"] = list(banks)
            ps_pool["n"] = 0

        def dma(q, out, in_, r, w, key=None):
            k = (q, "S", r[0]) if key == "ST" else (q, "L", w[0])
            return P.add(q, lambda e: e.dma_start(out=out, in_=in_), r=r, w=w, dma=k)

        def load_w(src_ap, ncols=512, nk=16):
            i = next_wb()
            dst = wbuf[i][:, 0:nk, 0:ncols]
            dma("pool", dst, src_ap.rearrange("(kc p) n -> p kc n", p=128), r=(), w=[("wb", i)])
            return i

        def mm_group(out, pairs, r, w):
            n = len(pairs)

            def fn(e):
                ins = None
                for j, (l, rr) in enumerate(pairs):
                    ins = e.matmul(out, lhsT=l, rhs=rr, start=(j == 0), stop=(j == n - 1))
                return ins
            return P.add("pe", fn, r=r, w=w)

        def transposes(items, r, w, ident):
            def fn(e):
                ins = None
                for o, i_ in items:
                    ins = e.transpose(o, i_, ident)
                return ins
            return P.add("pe", fn, r=r, w=w)

        def act(out, in_, func, r, w, scale=None, bias=None, accum=None):
            kw = {}
            if scale is not None:
                kw["scale"] = scale
            if bias is not None:
                kw["bias"] = bias
            if accum is not None:
                kw["accum_out"] = accum
            return P.add("act", lambda e: e.activation(out=out, in_=in_, func=func, **kw), r=r, w=w)

        def tt(eng, out, in0, in1, op, r, w):
            return P.add(eng, lambda e: e.tensor_tensor(out=out, in0=in0, in1=in1, op=op), r=r, w=w)

        def ts(eng, out, in0, s1, s2, op0, op1, r, w):
            return P.add(eng, lambda e: e.tensor_scalar(out=out, in0=in0, scalar1=s1, scalar2=s2, op0=op0, op1=op1), r=r, w=w)

        def stt(eng, out, in0, scalar, in1, op0, op1, r, w):
            return P.add(eng, lambda e: e.scalar_tensor_tensor(out=out, in0=in0, scalar=scalar, in1=in1, op0=op0, op1=op1), r=r, w=w)

        def cp(eng, out, in_, r, w):
            if eng == "act":
                return P.add("act", lambda e: e.activation(out=out, in_=in_, func=AF.Copy), r=r, w=w)
            return P.add(eng, lambda e: e.tensor_copy(out=out, in_=in_), r=r, w=w)

        def memset(out, val, w):
            return P.add("dve", lambda e: e.memset(out, val), r=(), w=w)

        def rstd_from(t, kt, lnexp):
            if lnexp:
                act(t, t, AF.Ln, r=kt, w=kt)
                act(t, t, AF.Exp, r=kt, w=kt, scale=-0.5)
            else:
                act(t, t, AF.Sqrt, r=kt, w=kt)
                P.add("dve", lambda e: e.reciprocal(out=t, in_=t), r=kt, w=kt)

        dma("sp", identf, consts[:, 0:128], r=(), w=["identf"])
        dma("sp", trif, consts[:, 128:256], r=(), w=["trif"])
        dma("sp", cpp, c_pp[:, :], r=(), w=["cpp"])
        cp("dve", identb, identf, r=["identf"], w=["identb"])
        memset(onesb, 1.0, ["onesb"])
        act(scpp, cpp, AF.Silu, r=["cpp"], w=["scpp"])
        cp("dve", screp, scpp.unsqueeze(2).to_broadcast([128, 16, 128]), r=["scpp"], w=["screp"])

        def ada_part(w_ap, b_ap, dst, kdst):
            dma("sp", tmpB, b_ap.partition_broadcast(128), r=(), w=[KB])
            for nb in range(4):
                wi = load_w(w_ap[:, nb * 512:(nb + 1) * 512])
                pb = next_ps()
                mm_group(PSF[pb], [(screp[:, kc, :], wbuf[wi][:, kc, :]) for kc in range(16)],
                         r=[("wb", wi), "screp"], w=[("ps", pb)])
                tt("dve", dst[:, nb * 512:(nb + 1) * 512], PSF[pb], tmpB[:, nb * 512:(nb + 1) * 512], ALU.add,
                   r=[("ps", pb), KB], w=[kdst])

        def diag_extract(src, dst, ksrc, kdst):
            v = src.rearrange("p (a b) -> p a b", a=16)
            tt("dve", v, v, identf.unsqueeze(1).to_broadcast([128, 16, 128]), ALU.mult, r=[ksrc, "identf"], w=[ksrc])
            P.add("dve", lambda e: e.tensor_reduce(out=dst, in_=v, axis=AX.X, op=ALU.add), r=[ksrc], w=[kdst])

        def ada_layer(w_ap, b_ap, g_ap, with_gate):
            ada_part(w_ap[:, 0:D], b_ap[:, 0:D], tmpS, KS)
            diag_extract(tmpS, Bpp, KS, "Bpp")
            ada_part(w_ap[:, D:2 * D], b_ap[:, D:2 * D], tmpS, KS)
            dma("sp", tmpB, g_ap.partition_broadcast(128), r=(), w=[KB])
            stt("dve", tmpS, tmpS, 1.0, tmpB, ALU.add, ALU.mult, r=[KS, KB], w=[KS])
            diag_extract(tmpS, App, KS, "App")
            if with_gate:
                ada_part(w_ap[:, 2 * D:3 * D], b_ap[:, 2 * D:3 * D], gate_bc, "gate")

        SS4 = [("ss4", m) for m in range(4)]

        def stage_n(xsrc, ti, lnexp):
            memset(ss4, 0.0, SS4)
            xs = []
            for m in range(4):
                xi = next_xio()
                blk = ti * 4 + m
                dma("sp", xio[xi], xsrc[blk * 128:(blk + 1) * 128, :], r=[("xd", id(xsrc), ti, nb) for nb in range(4)],
                    w=[("xio", xi)])
                act(J["junk"], xio[xi], AF.Square, r=[("xio", xi)], w=[("ss4", m)] + J["keys"], accum=ss4[:, m:m + 1])
                xs.append(xi)
            ts("dve", rs4, ss4, 1.0 / D, EPS, ALU.mult, ALU.add, r=SS4, w=["rs4"])
            rstd_from(rs4, ["rs4"], lnexp)
            for m in range(4):
                xi = xs[m]
                sb = st["xsb"] % 2
                st["xsb"] += 1
                act(xsb[sb], xio[xi], AF.Identity, r=[("xio", xi), "rs4"], w=[("xsb", sb)], scale=rs4[:, m:m + 1])
                for half in range(2):
                    pb = next_ps()
                    pv = PSB[pb].rearrange("p (a b) -> p a b", a=8)
                    transposes([(pv[:, k, :], xsb[sb][:, (half * 8 + k) * 128:(half * 8 + k + 1) * 128]) for k in range(8)],
                               r=[("xsb", sb), "identb"], w=[("ps", pb)], ident=identb)
                    for k in range(8):
                        kc = half * 8 + k
                        act(hT[:, kc, m * 128:(m + 1) * 128], pv[:, k, :], AF.Identity,
                            r=[("ps", pb), "App", "Bpp"], w=[("hT", m)],
                            scale=App[:, kc:kc + 1], bias=Bpp[:, kc:kc + 1])

        HT_ALL = [("hT", m) for m in range(4)]

        def residual(xsrc, xdst, ti, w_ap, nk, lhs_of, r_lhs):
            def ld(nb):
                xi = next_xio()
                xv = xio[xi].rearrange("p (m c) -> p m c", m=4)
                src = xsrc[ti * 512:(ti + 1) * 512, nb * 512:(nb + 1) * 512].rearrange("(m p) c -> p m c", p=128)
                dma("sp", xv, src, r=[("xd", id(xsrc), ti, nb)], w=[("xio", xi)])
                return xi, xv
            nxt = ld(0)
            for nb in range(4):
                xi, xv = nxt
                if nb < 3:
                    nxt = ld(nb + 1)
                banks = [next_ps() for _ in range(4)]
                nh = nk // 16
                for h in range(nh):
                    wi = load_w(w_ap[h * 2048:(h + 1) * 2048, nb * 512:(nb + 1) * 512])
                    for m in range(4):
                        def fn(e, m=m, h=h, wi=wi, pb=banks[m]):
                            ins = None
                            for k in range(16):
                                kc = h * 16 + k
                                ins = e.matmul(PSF[pb], lhsT=lhs_of(kc, m), rhs=wbuf[wi][:, k, :],
                                               start=(kc == 0), stop=(kc == nk - 1))
                            return ins
                        P.add("pe", fn, r=[("wb", wi)] + r_lhs, w=[("ps", banks[m])])
                for m in range(4):
                    pb = banks[m]
                    ri = st["rt"] % 2
                    st["rt"] += 1
                    tt("dve", rtmp[ri], PSF[pb], gate_bc[:, nb * 512:(nb + 1) * 512], ALU.mult,
                       r=[("ps", pb), "gate"], w=[("xsb", 0)])
                    tt("dve", xv[:, m, :], xv[:, m, :], rtmp[ri], ALU.add,
                       r=[("xsb", 0), ("xio", xi)], w=[("xio", xi)])
                dst = xdst[ti * 512:(ti + 1) * 512, nb * 512:(nb + 1) * 512].rearrange("(m p) c -> p m c", p=128)
                dma("sp", dst, xv, r=[("xio", xi)], w=[("xd", id(xdst), ti, nb)], key="ST")

        ntA = 2 * ntiles if fused else ntiles
        nA = dbg.get("nA", 2) if dbg else 2
        doKV = dbg.get("kv", True) if dbg else True
        nB = dbg.get("nB", 2) if dbg else 2

        if doA:
            AR.reset(persist_mark)
            uz = AR.get([32, 512], BF16)
            vraw = AR.get([4, 4096], BF16)
            lb2 = AR.get([4096], BF16, parts=2)
            rb2 = AR.get([16, 128], BF16, parts=2)
            wsT = AR.get([16, 128], BF16)
            wsf = [AR.get([128], F32) for _ in range(2)]
            gt_all = AR.get([D], BF16)
            gtmp = [gt_all[:, s_ * 512:(s_ + 1) * 512] for s_ in range(4)]
            J["junk"] = gt_all
            J["keys"] = [("gtmp", s_) for s_ in range(4)]
            bst = AR.get([8, 6], F32)
            mv4 = AR.get([4, 2], F32)
            vrs = AR.get([4, 2], F32)

            for l in range(nA):
                P.barrier()
                set_ps_banks(range(8))
                xsrc = x_in if l == 0 else xA
                xdst = xA if l == 0 else xB
                ada_layer(ada_w[l], ada_b[l:l + 1, :], norm_g[l:l + 1, :], True)
                memset(lb2[0:2, :], 1.0, ["lb2"])
                dma("pool", lb2[0:1, :], a_ln_b[l:l + 1, :], r=(), w=["lb2"])
                dma("pool", rb2[1:2, :, :], a_b_s[l:l + 1, :, :], r=(), w=["rb2b"])
                for g in range(16):
                    wf = wsf[g % 2]
                    kf = ("wsf", g % 2)
                    dma("sp", wf, a_w_s[l, g, :, :], r=(), w=[kf])
                    pb = next_ps()
                    transposes([(PSF[pb][:, 0:128], wf)], r=[kf, "identf"], w=[("ps", pb)], ident=identf)
                    tt("dve", wsT[:, g, :], PSF[pb][:, 0:128], trif, ALU.mult, r=[("ps", pb), "trif"], w=[("wsT", g)])
                    pb2 = next_ps()
                    mm_group(PSF[pb2][0:1, 0:128], [(onesb[:, 0:1], wsT[:, g, :])], r=[("wsT", g), "onesb"], w=[("ps", pb2)])
                    cp("dve", rb2[0:1, g, :], PSF[pb2][0:1, 0:128], r=[("ps", pb2)], w=[("rb2a", g)])
                WS_ALL = [("wsT", g) for g in range(16)] + [("rb2a", g) for g in range(16)] + ["rb2b", "lb2"]

                for ti in range(ntA):
                    stage_n(xsrc, ti, False)
                    w_in = a_w_in[l]
                    for gi in range(8):
                        wi = load_w(w_in[:, gi * 512:(gi + 1) * 512])
                        for s in range(4):
                            cc = gi * 4 + s
                            pb = next_ps()
                            mm_group(PSF[pb], [(wbuf[wi][:, kc, s * 128:(s + 1) * 128], hT[:, kc, :]) for kc in range(16)],
                                     r=[("wb", wi)] + HT_ALL, w=[("ps", pb)])
                            act(uz[:, cc, :], PSF[pb], AF.Gelu_apprx_tanh, r=[("ps", pb)], w=[("uz", cc)])
                    for gi in range(8):
                        wi = load_w(w_in[:, 4096 + gi * 512: 4096 + (gi + 1) * 512])
                        for m in range(4):
                            pb = next_ps()
                            mm_group(PSF[pb], [(hT[:, kc, m * 128:(m + 1) * 128], wbuf[wi][:, kc, :]) for kc in range(16)],
                                     r=[("wb", wi), ("hT", m)], w=[("ps", pb)])
                            act(vraw[:, m, gi * 512:(gi + 1) * 512], PSF[pb], AF.Gelu_apprx_tanh,
                                r=[("ps", pb)], w=[("vraw", m)])
                    for gi in range(8):
                        wi = load_w(w_in[:, 8192 + gi * 512: 8192 + (gi + 1) * 512])
                        for s in range(4):
                            cc = gi * 4 + s
                            pb = next_ps()
                            mm_group(PSF[pb], [(wbuf[wi][:, kc, s * 128:(s + 1) * 128], hT[:, kc, :]) for kc in range(16)],
                                     r=[("wb", wi)] + HT_ALL, w=[("ps", pb)])
                            act(gtmp[s], PSF[pb], AF.Silu, r=[("ps", pb)], w=[("gtmp", s)])
                            tt("dve", uz[:, cc, :], uz[:, cc, :], gtmp[s], ALU.mult,
                               r=[("uz", cc), ("gtmp", s)], w=[("uz", cc)])
                    lgi = next_xio()
                    lng_bc = xio[lgi].bitcast(BF16)
                    klg = ("xio", lgi)
                    dma("pool", lng_bc, a_ln_g[l:l + 1, :].partition_broadcast(128), r=(), w=[klg])
                    for m in range(4):
                        vm = vraw[:, m, :]
                        def fbn(e, vm=vm):
                            ins = None
                            for k in range(8):
                                ins = e.bn_stats(out=bst[:, k, :], in_=vm[:, k * 512:(k + 1) * 512])
                            return ins
                        P.add("dve", fbn, r=[("vraw", m)], w=["bst"])
                        P.add("dve", lambda e, m=m: e.bn_aggr(out=mv4[:, m, :], in_=bst.rearrange("p a b -> p (a b)")),
                              r=["bst"], w=[("mv4", m)])
                    MV = [("mv4", m) for m in range(4)]
                    ts("dve", vrs[:, :, 0], mv4[:, :, 1], 1.0, EPS, ALU.mult, ALU.add, r=MV, w=["vrs0"])
                    rstd_from(vrs[:, :, 0], ["vrs0"], False)
                    stt("dve", vrs[:, :, 1], mv4[:, :, 0], -1.0, vrs[:, :, 0], ALU.mult, ALU.mult, r=MV + ["vrs0"], w=["vrs1"])
                    for m in range(4):
                        vm = vraw[:, m, :]
                        act(vm, vm, AF.Identity, r=[("vraw", m), "vrs0", "vrs1"], w=[("vraw", m)],
                            scale=vrs[:, m, 0:1], bias=vrs[:, m, 1:2])
                        tt("dve", vm, vm, lng_bc, ALU.mult, r=[("vraw", m), klg], w=[("vraw", m)])
                    for m in range(4):
                        for q4 in range(8):
                            pb = next_ps()
                            pv = PSF[pb].rearrange("p (a b) -> p a b", a=4)

                            def fn(e, m=m, q4=q4, pv=pv):
                                ins = None
                                for s in range(4):
                                    cc = q4 * 4 + s
                                    g = cc // 2
                                    e.matmul(pv[:, s, :], lhsT=vraw[:, m, cc * 128:(cc + 1) * 128], rhs=wsT[:, g, :],
                                             start=True, stop=False)
                                    ins = e.matmul(pv[:, s, :], lhsT=lb2[0:2, cc * 128:(cc + 1) * 128], rhs=rb2[0:2, g, :],
                                                   start=False, stop=True)
                                return ins
                            P.add("pe", fn, r=[("vraw", m)] + WS_ALL, w=[("ps", pb)])
                            yv = uz[:, q4 * 4:(q4 + 1) * 4, m * 128:(m + 1) * 128]
                            ku = [("uz", q4 * 4 + s) for s in range(4)]
                            tt("dve", yv, pv, yv, ALU.mult, r=[("ps", pb)] + ku, w=ku)
                    residual(xsrc, xdst, ti, a_w_out[l], 32,
                             lambda kc, m: uz[:, kc, m * 128:(m + 1) * 128], [("uz", cc) for cc in range(32)])

        if doA and doKV:
            P.barrier()
            AR.reset(persist_mark)
            set_ps_banks(range(8))
            rope_t = AR.get([2, 4, 32], F32)
            wdkv = AR.get([16, 576], BF16)
            wukv = AR.get([4, 4096], BF16)
            gkva = AR.get([4], F32)
            gkn_bc = AR.get([128], F32)
            gkr_bc = AR.get([64], F32)
            ckvn = AR.get([512], BF16)
            ckvT = AR.get([4, 128], BF16)
            kraw = AR.get([16, 128], F32)
            J["junk"] = kraw.rearrange("p a b -> p (a b)").bitcast(BF16)[:, 0:D]
            J["keys"] = ["kraw"]
            kn = AR.get([16, 128], BF16)
            vst = AR.get([16, 128], BF16)
            kTst = AR.get([16, 128], BF16)
            sq = AR.get([512], F32)
            ssk = AR.get([16, 2], F32)
            rsk = AR.get([16], F32)
            ssc = AR.get([2], F32)
            rsc = AR.get([2], F32)
            krn = AR.get([64], F32)
            krot = AR.get([64], BF16)
            rt4 = AR.get([4, 32], F32)
            krTst = AR.get([128], BF16)

            ada_layer(kv_ada_w, kv_ada_b, kv_norm_g, False)
            dma("pool", wdkv, kv_w_dkv.rearrange("(kc p) n -> p kc n", p=128), r=(), w=["wdkv"])
            dma("pool", wukv, kv_w_ukv.rearrange("(kc p) n -> p kc n", p=128), r=(), w=["wukv"])
            dma("sp", gkva, kv_g_kva[:, :], r=(), w=["gkva"])
            dma("sp", gkn_bc, kv_g_kn.partition_broadcast(128), r=(), w=["gkn"])
            dma("sp", gkr_bc, kv_g_kr.partition_broadcast(128), r=(), w=["gkr"])
            if fused:
                kT_v2 = kT_g.rearrange("(r h d) t -> r d h t", r=2, h=16)
                krT_v2 = krT_g.rearrange("(r d) t -> r d t", r=2)
                v_v2 = v_g.rearrange("(r h t) (j d) -> r t h j d", r=2, h=16, d=128)
            else:
                kT_v = kT_loc.rearrange("(h d) t -> d h t", d=128)
                v_v = v_loc.rearrange("(h t) (j d) -> t h j d", t=128, d=128)

            xkv = {0: x_in, 1: xA, 2: xB}[nA]
            for ti in range(ntA):
                stage_n(xkv, ti, False)
                dma("sp", rope_t, rope[:, :, ti * 4:(ti + 1) * 4, :], r=(), w=["rope"])
                for m in range(4):
                    blk = ti * 4 + m
                    rr, jl = blk // NBLK, blk % NBLK
                    msl = slice(m * 128, (m + 1) * 128)
                    pbc = next_ps()
                    mm_group(PSF[pbc], [(hT[:, kc, msl], wdkv[:, kc, 0:512]) for kc in range(16)],
                             r=[("hT", m), "wdkv"], w=[("ps", pbc)])
                    pbr = next_ps()
                    mm_group(PSF[pbr][:, 0:64], [(hT[:, kc, msl], wdkv[:, kc, 512:576]) for kc in range(16)],
                             r=[("hT", m), "wdkv"], w=[("ps", pbr)])
                    memset(ssc, 0.0, ["ssc0", "ssc1"])
                    act(J["junk"][:, 0:512], PSF[pbc], AF.Square, r=[("ps", pbc)], w=["ssc0"] + J["keys"], accum=ssc[:, 0:1])
                    act(J["junk"][:, 0:64], PSF[pbr][:, 0:64], AF.Square, r=[("ps", pbr)], w=["ssc1"] + J["keys"], accum=ssc[:, 1:2])
                    ts("dve", rsc[:, 0:1], ssc[:, 0:1], 1.0 / 512, EPS, ALU.mult, ALU.add, r=["ssc0"], w=["rsc"])
                    ts("dve", rsc[:, 1:2], ssc[:, 1:2], 1.0 / 64, EPS, ALU.mult, ALU.add, r=["ssc1"], w=["rsc"])
                    rstd_from(rsc, ["rsc"], False)
                    act(ckvn, PSF[pbc], AF.Identity, r=[("ps", pbc), "rsc"], w=["ckvn"], scale=rsc[:, 0:1])
                    pt = next_ps()
                    ptv = PSB[pt].rearrange("p (a b) -> p a b", a=8)
                    transposes([(ptv[:, k, :], ckvn[:, k * 128:(k + 1) * 128]) for k in range(4)],
                               r=["ckvn", "identb"], w=[("ps", pt)], ident=identb)
                    tt("dve", ckvT, ptv[:, 0:4, :], gkva.unsqueeze(2).to_broadcast([128, 4, 128]), ALU.mult,
                       r=[("ps", pt), "gkva"], w=["ckvT"])
                    stt("dve", krn, PSF[pbr][:, 0:64], rsc[:, 1:2], gkr_bc, ALU.mult, ALU.mult,
                        r=[("ps", pbr), "rsc", "gkr"], w=["krn"])
                    cs = rope_t[:, 0, m, :]
                    sn = rope_t[:, 1, m, :]
                    tt("dve", rt4[:, 0, :], krn[:, 0:32], cs, ALU.mult, r=["krn", "rope"], w=["rt0"])
                    tt("dve", rt4[:, 1, :], krn[:, 32:64], sn, ALU.mult, r=["krn", "rope"], w=["rt1"])
                    tt("dve", rt4[:, 2, :], krn[:, 32:64], cs, ALU.mult, r=["krn", "rope"], w=["rt2"])
                    tt("dve", rt4[:, 3, :], krn[:, 0:32], sn, ALU.mult, r=["krn", "rope"], w=["rt3"])
                    tt("dve", krot[:, 0:32], rt4[:, 0, :], rt4[:, 1, :], ALU.subtract, r=["rt0", "rt1"], w=["krot"])
                    tt("dve", krot[:, 32:64], rt4[:, 2, :], rt4[:, 3, :], ALU.add, r=["rt2", "rt3"], w=["krot"])
                    pt2 = next_ps()
                    transposes([(PSB[pt2][0:64, 0:128], krot)], r=["krot", "identb"], w=[("ps", pt2)], ident=identb)
                    cp("act", krTst[0:64, :], PSB[pt2][0:64, 0:128], r=[("ps", pt2)], w=["krTst"])
                    krdst = krT_v2[rr][:, jl * 128:(jl + 1) * 128] if fused else krT_loc[:, blk * 128:(blk + 1) * 128]
                    dma("sp", krdst, krTst[0:64, :], r=["krTst"], w=[("krTd", blk)], key="ST")
                    memset(ssk, 0.0, ["ssk"])
                    for g in range(8):
                        pb = next_ps()
                        mm_group(PSF[pb], [(ckvT[:, kc, :], wukv[:, kc, g * 512:(g + 1) * 512]) for kc in range(4)],
                                 r=["ckvT", "wukv"], w=[("ps", pb)])
                        act(sq, PSF[pb], AF.Square, r=[("ps", pb)], w=["sq"])
                        P.add("dve", lambda e, g=g: e.tensor_reduce(out=ssk[:, 2 * g:2 * g + 2, :],
                                                                    in_=sq.rearrange("p (a b c) -> p a b c", a=2, b=2),
                                                                    axis=AX.X, op=ALU.add),
                              r=["sq"], w=["ssk"])
                        pv4 = PSF[pb].rearrange("p (a b c) -> p a b c", a=2, b=2)
                        cp("dve", kraw[:, 2 * g:2 * g + 2, :], pv4[:, :, 0, :], r=[("ps", pb)], w=["kraw"])
                        cp("act", vst[:, 2 * g:2 * g + 2, :], pv4[:, :, 1, :], r=[("ps", pb)], w=["vst"])
                    ts("dve", rsk, ssk[:, :, 0], 1.0 / 128, EPS, ALU.mult, ALU.add, r=["ssk"], w=["rsk"])
                    rstd_from(rsk, ["rsk"], False)
                    tt("dve", kraw, kraw, rsk.unsqueeze(2).to_broadcast([128, 16, 128]), ALU.mult, r=["kraw", "rsk"], w=["kraw"])
                    tt("dve", kn, kraw, gkn_bc.unsqueeze(1).to_broadcast([128, 16, 128]), ALU.mult, r=["kraw", "gkn"], w=["kn"])
                    for half in range(2):
                        pb = next_ps()
                        pv = PSB[pb].rearrange("p (a b) -> p a b", a=8)
                        transposes([(pv[:, k, :], kn[:, half * 8 + k, :]) for k in range(8)],
                                   r=["kn", "identb"], w=[("ps", pb)], ident=identb)
                        cp("act", kTst[:, half * 8:(half + 1) * 8, :], pv, r=[("ps", pb)], w=["kTst"])
                    if fused:
                        dma("sp", kT_v2[rr][:, :, jl * 128:(jl + 1) * 128], kTst, r=["kTst"], w=[("kTd", blk)], key="ST")
                        dma("sp", v_v2[rr][:, :, jl, :], vst, r=["vst"], w=[("vd", blk)], key="ST")
                    else:
                        dma("sp", kT_v[:, :, blk * 128:(blk + 1) * 128], kTst, r=["kTst"], w=[("kTd", blk)], key="ST")
                        dma("sp", v_v[:, :, blk, :], vst, r=["vst"], w=[("vd", blk)], key="ST")

        if doB:
            P.barrier()
            AR.reset(persist_mark)
            rope_t = AR.get([2, NBLK, 32], F32)
            J["junk"] = AR.get([D], BF16)
            J["keys"] = ["junk"]
            dma("sp", rope_t, rope[:, :, 0:NBLK, :], r=(), w=["rope"])
            cqT = AR.get([4, 512], BF16)
            zT = AR.get([16, 512], BF16)
            cqn = [AR.get([512], BF16) for _ in range(2)]
            wuq = [AR.get([4, 192], BF16) for _ in range(2)]
            NKV = 4
            kvc = [(AR.get([512], BF16), AR.get([512], BF16), AR.get([4, 128], BF16)) for _ in range(NKV)]
            masks = AR.get([8, 512], BF16)
            qraw = AR.get([4, 192], F32)
            ssq = AR.get([4, 2], F32)
            rsq = AR.get([4, 2], F32)
            sscq = AR.get([4], F32)
            rscq = AR.get([4], F32)
            gqa = AR.get([4], F32)
            gqn_bc = AR.get([128], F32)
            gqr_bc = AR.get([64], F32)
            qn = AR.get([4, 128], BF16)
            qr = AR.get([4, 64], F32)
            qrot = AR.get([4, 64], BF16)
            rq4 = AR.get([4, 4, 32], F32)
            QTn = [AR.get([512], BF16) for _ in range(2)]
            QTr = [AR.get([512], BF16) for _ in range(2)]
            pT = [AR.get([512], BF16) for _ in range(3)]
            rinv = [AR.get([512], F32) for _ in range(2)]
            otmp = [AR.get([512], F32) for _ in range(2)]

            dma("sp", masks, amask[:, :, :], r=(), w=["masks"])
            krT_gv = krT_g.rearrange("(r d) t -> r d t", r=2)
            kT_gv = kT_g.rearrange("(r h d) t -> r h d t", r=2, h=16)
            v_gv = v_g.rearrange("(r h t) (j d) -> r h t j d", r=2, h=16, d=128)
            k_src = lambda r_, hd_: kT_gv[r_, hd_]
            v_src = lambda r_, hd_: v_gv[r_, hd_]
            cnt = {"kv": 0, "s": 0, "p": 0, "q": 0}

            for lb in range(nB):
                l = 2 + lb
                P.barrier()
                set_ps_banks(range(8))
                xsrc = xB if lb == 0 else xA
                xdst = xA if lb == 0 else y_out
                if nB == 1:
                    xdst = y_out
                ada_layer(ada_w[l], ada_b[l:l + 1, :], norm_g[l:l + 1, :], True)
                dma("sp", gqa, b_g_qa[lb, :, :], r=(), w=["gqa"])
                dma("sp", gqn_bc, b_g_qn[lb:lb + 1, :].partition_broadcast(128), r=(), w=["gqn"])
                dma("sp", gqr_bc, b_g_qr[lb:lb + 1, :].partition_broadcast(128), r=(), w=["gqr"])
                w_in = b_w_in[lb]

                for ti in range(ntiles):
                    set_ps_banks(range(8))
                    stage_n(xsrc, ti, True)
                    wi = load_w(w_in[:, 0:512])
                    memset(sscq, 0.0, [("sscq", m) for m in range(4)])
                    cb = []
                    for m in range(4):
                        pb = next_ps()
                        cb.append(pb)
                        mm_group(PSF[pb], [(hT[:, kc, m * 128:(m + 1) * 128], wbuf[wi][:, kc, :]) for kc in range(16)],
                                 r=[("wb", wi), ("hT", m)], w=[("ps", pb)])
                        act(J["junk"][:, 0:512], PSF[pb], AF.Square, r=[("ps", pb)], w=[("sscq", m)] + J["keys"], accum=sscq[:, m:m + 1])
                    ts("dve", rscq, sscq, 1.0 / 512, EPS, ALU.mult, ALU.add, r=[("sscq", m) for m in range(4)], w=["rscq"])
                    rstd_from(rscq, ["rscq"], True)
                    for m in range(4):
                        pb = cb[m]
                        ci = m % 2
                        act(cqn[ci], PSF[pb], AF.Identity, r=[("ps", pb), "rscq"], w=[("cqn", ci)], scale=rscq[:, m:m + 1])
                        pt = next_ps()
                        ptv = PSB[pt].rearrange("p (a b) -> p a b", a=8)
                        transposes([(ptv[:, k, :], cqn[ci][:, k * 128:(k + 1) * 128]) for k in range(4)],
                                   r=[("cqn", ci), "identb"], w=[("ps", pt)], ident=identb)
                        tt("dve", cqT[:, :, m * 128:(m + 1) * 128], ptv[:, 0:4, :],
                           gqa.unsqueeze(2).to_broadcast([128, 4, 128]), ALU.mult,
                           r=[("ps", pt), "gqa"], w=[("cqT", m)])
                    for g in range(4):
                        wi = load_w(w_in[:, 512 + g * 512: 512 + (g + 1) * 512])
                        for s in range(4):
                            hd = g * 4 + s
                            pb = next_ps()
                            mm_group(PSF[pb], [(wbuf[wi][:, kc, s * 128:(s + 1) * 128], hT[:, kc, :]) for kc in range(16)],
                                     r=[("wb", wi)] + HT_ALL, w=[("ps", pb)])
                            act(zT[:, hd, :], PSF[pb], AF.Silu, r=[("ps", pb)], w=[("zT", hd)])
                    set_ps_banks([7])
                    for hd in range(16):
                        qi = cnt["q"] % 2
                        cnt["q"] += 1
                        dma("pool", wuq[qi], b_w_uq[lb][:, hd * 192:(hd + 1) * 192].rearrange("(kc p) n -> p kc n", p=128),
                            r=(), w=[("wuq", qi)])
                        memset(ssq, 0.0, ["ssq"])
                        for m in range(4):
                            mm_group(PSF[7][:, 0:192], [(cqT[:, kc, m * 128:(m + 1) * 128], wuq[qi][:, kc, :]) for kc in range(4)],
                                     r=[("cqT", m), ("wuq", qi)], w=[("ps", 7)])
                            cp("dve", qraw[:, m, :], PSF[7][:, 0:192], r=[("ps", 7)], w=[("qraw", m)])
                            act(J["junk"][:, 0:128], qraw[:, m, 0:128], AF.Square, r=[("qraw", m)], w=["ssq"] + J["keys"], accum=ssq[:, m, 0:1])
                            act(J["junk"][:, 0:64], qraw[:, m, 128:192], AF.Square, r=[("qraw", m)], w=["ssq"] + J["keys"], accum=ssq[:, m, 1:2])
                        ts("dve", rsq[:, :, 0], ssq[:, :, 0], 1.0 / 128, EPS, ALU.mult, ALU.add, r=["ssq"], w=["rsq"])
                        ts("dve", rsq[:, :, 1], ssq[:, :, 1], 1.0 / 64, EPS, ALU.mult, ALU.add, r=["ssq"], w=["rsq"])
                        rstd_from(rsq, ["rsq"], True)
                        for m in range(4):
                            stt("dve", qn[:, m, :], qraw[:, m, 0:128], rsq[:, m, 0:1], gqn_bc, ALU.mult, ALU.mult,
                                r=[("qraw", m), "rsq", "gqn"], w=["qn"])
                            stt("dve", qr[:, m, :], qraw[:, m, 128:192], rsq[:, m, 1:2], gqr_bc, ALU.mult, ALU.mult,
                                r=[("qraw", m), "rsq", "gqr"], w=["qr"])
                        cs = rope_t[:, 0, ti * 4:(ti + 1) * 4, :]
                        sn = rope_t[:, 1, ti * 4:(ti + 1) * 4, :]
                        tt("dve", rq4[:, 0, :, :], qr[:, :, 0:32], cs, ALU.mult, r=["qr", "rope"], w=["rq0"])
                        tt("dve", rq4[:, 1, :, :], qr[:, :, 32:64], sn, ALU.mult, r=["qr", "rope"], w=["rq1"])
                        tt("dve", rq4[:, 2, :, :], qr[:, :, 32:64], cs, ALU.mult, r=["qr", "rope"], w=["rq2"])
                        tt("dve", rq4[:, 3, :, :], qr[:, :, 0:32], sn, ALU.mult, r=["qr", "rope"], w=["rq3"])
                        tt("dve", qrot[:, :, 0:32], rq4[:, 0, :, :], rq4[:, 1, :, :], ALU.subtract, r=["rq0", "rq1"], w=["qrot"])
                        tt("dve", qrot[:, :, 32:64], rq4[:, 2, :, :], rq4[:, 3, :, :], ALU.add, r=["rq2", "rq3"], w=["qrot"])
                        pq = PSB[7]
                        transposes([(pq[:, m * 128:(m + 1) * 128], qn[:, m, :]) for m in range(4)]
                                   + [(pq[0:64, 512 + m * 128:512 + (m + 1) * 128], qrot[:, m, :]) for m in range(4)],
                                   r=["qn", "qrot", "identb"], w=[("ps", 7)], ident=identb)
                        cp("act", QTn[qi], pq[:, 0:512], r=[("ps", 7)], w=[("QTn", qi)])
                        cp("act", QTr[qi][0:64, :], pq[0:64, 512:1024], r=[("ps", 7)], w=[("QTr", qi)])
                        ob = 3 + hd % 2
                        lbk = 5 + hd % 2
                        chunks = [(r, jc) for r in range(2) for jc in range(ti + 1)]
                        nkt = len(chunks) * 4
                        kti = 0
                        for (r, jc) in chunks:
                            ki = cnt["kv"] % NKV
                            cnt["kv"] += 1
                            kcb, krb, vcb = kvc[ki]
                            tsl = slice(jc * 512, (jc + 1) * 512)
                            dma("sp", kcb, k_src(r, hd)[:, tsl], r=(), w=[("kvc", ki)])
                            dma("sp", krb[0:64, :], krT_gv[r, :, tsl], r=(), w=[("kvr", ki)])
                            dma("sp", vcb, v_src(r, hd)[:, jc * 4:(jc + 1) * 4, :], r=(), w=[("kvv", ki)])
                            for kt in range(4):
                                sb = cnt["s"] % 3
                                cnt["s"] += 1
                                pairs = [(kcb[:, kt * 128:(kt + 1) * 128], QTn[qi]),
                                         (krb[0:64, kt * 128:(kt + 1) * 128], QTr[qi][0:64, :])]
                                rk = [("kvc", ki), ("kvr", ki), ("QTn", qi), ("QTr", qi)]
                                if jc == ti:
                                    pairs.append((identb, masks[:, r * 4 + kt, :]))
                                    rk += ["identb", "masks"]
                                mm_group(PSF[sb], pairs, r=rk, w=[("ps", sb)])
                                pi = cnt["p"] % 3
                                cnt["p"] += 1
                                act(pT[pi], PSF[sb], AF.Exp, r=[("ps", sb)], w=[("pT", pi)], scale=SM_SCALE)

                                def fn(e, vcb=vcb, kt=kt, pi=pi, ob=ob, lbk=lbk, first=(kti == 0), last=(kti == nkt - 1)):
                                    e.matmul(PSF[ob], lhsT=vcb[:, kt, :], rhs=pT[pi], start=first, stop=last)
                                    return e.matmul(PSF[lbk], lhsT=onesb, rhs=pT[pi], start=first, stop=last)
                                P.add("pe", fn, r=[("kvv", ki), ("pT", pi), "onesb"], w=[("ps", ob), ("ps", lbk)])
                                kti += 1
                        oi = hd % 2
                        P.add("dve", lambda e, oi=oi, lbk=lbk: e.reciprocal(out=rinv[oi], in_=PSF[lbk]),
                              r=[("ps", lbk)], w=[("rinv", oi)])
                        tt("dve", otmp[oi], PSF[ob], rinv[oi], ALU.mult, r=[("ps", ob), ("rinv", oi)], w=[("otmp", oi)])
                        tt("dve", zT[:, hd, :], otmp[oi], zT[:, hd, :], ALU.mult, r=[("otmp", oi), ("zT", hd)], w=[("zT", hd)])
                    set_ps_banks(range(8))
                    residual(xsrc, xdst, ti, b_w_out[lb], 16,
                             lambda kc, m: zT[:, kc, m * 128:(m + 1) * 128], [("zT", hd) for hd in range(16)])

        print("SBUF arena high-water bytes/partition:", AR.hi)
        P.finalize(nc, stack)
    build.last_arena_log = AR.log
    return nc


def _core_consts(par, rel=False):
    ident = np.eye(128, dtype=np.float32)
    tri = (np.arange(128)[:, None] <= np.arange(128)[None, :]).astype(np.float32)
    consts = np.concatenate([ident, tri, np.zeros((128, 128), np.float32)], axis=1)
    blk = np.arange(NBLK)
    pos = ((2 * blk[:, None] + par) * 128 + np.arange(128)[None, :]).astype(np.float32)
    if rel:
        pos = np.concatenate([pos, ((2 * blk[:, None] + (1 - par)) * 128 + np.arange(128)[None, :]).astype(np.float32)], axis=0)
    inv_freq = (np.float32(10000.0) ** (-np.arange(0, 64, 2, dtype=np.float32) / np.float32(64))).astype(np.float32)
    ang = (pos[:, :, None] * inv_freq[None, None, :]).astype(np.float32)
    rope = np.stack([np.cos(ang), np.sin(ang)], axis=0).astype(np.float32)
    rope = np.ascontiguousarray(rope.transpose(2, 0, 1, 3))
    tk = np.arange(128)[:, None, None, None, None]
    r = np.arange(2)[None, :, None, None, None]
    jj = np.arange(4)[None, None, :, None, None]
    m = np.arange(4)[None, None, None, :, None]
    tq = np.arange(128)[None, None, None, None, :]
    pk = np.where(r == 0, par, 1 - par) if rel else r
    vis = ((2 * jj + pk) * 128 + tk) <= ((2 * m + par) * 128 + tq)
    amask = np.where(vis, 0.0, NEG).astype(np.float32).reshape(128, 8, 512).astype(ml_dtypes.bfloat16)
    return consts, rope, amask


def _shard_x(x, b, par):
    return np.ascontiguousarray(x[b].reshape(64, 128, D)[par::2].reshape(TOK, D))


def _common_inputs(inp, b, par, rel=False):
    consts, rope, amask = _core_consts(par, rel)
    f = lambda a: np.ascontiguousarray(np.asarray(a, dtype=np.float32))
    d = {
        "c_pp": np.ascontiguousarray(f(inp["c"])[b].reshape(16, 128).T),
        "consts": consts, "rope": rope,
        "ada_w": f(inp["ada_w"]), "ada_b": f(inp["ada_b"]), "norm_g": f(inp["norm_g"]),
    }
    return d, amask


def _a_inputs(inp):
    f = lambda a: np.ascontiguousarray(np.asarray(a, dtype=np.float32))
    return {
        "a_w_in": f(inp["a_w_in"]), "a_ln_g": f(inp["a_ln_g"]), "a_ln_b": f(inp["a_ln_b"]),
        "a_w_s": f(inp["a_w_s"]), "a_b_s": f(inp["a_b_s"]), "a_w_out": f(inp["a_w_out"]),
        "kv_ada_w": f(inp["kv_ada_w"]), "kv_ada_b": f(inp["kv_ada_b"]).reshape(1, -1),
        "kv_norm_g": f(inp["kv_norm_g"]).reshape(1, -1), "kv_w_dkv": f(inp["kv_w_dkv"]),
        "kv_g_kva": np.ascontiguousarray(f(inp["kv_g_kva"]).reshape(4, 128).T),
        "kv_w_ukv": f(inp["kv_w_ukv"]), "kv_g_kn": f(inp["kv_g_kn"]).reshape(1, -1),
        "kv_g_kr": f(inp["kv_g_kr"]).reshape(1, -1),
    }


def _b_inputs(inp, amask):
    f = lambda a: np.ascontiguousarray(np.asarray(a, dtype=np.float32))
    return {
        "b_w_in": f(inp["b_w_in"]),
        "b_g_qa": np.ascontiguousarray(f(inp["b_g_qa"]).reshape(2, 4, 128).transpose(0, 2, 1)),
        "b_w_uq": f(inp["b_w_uq"]), "b_g_qn": f(inp["b_g_qn"]), "b_g_qr": f(inp["b_g_qr"]),
        "b_w_out": f(inp["b_w_out"]), "amask": amask,
    }


_NC_CACHE = {}


def _get_nc(mode):
    if mode not in _NC_CACHE:
        _NC_CACHE[mode] = build(mode)
    return _NC_CACHE[mode]


FUSED = True


def kernel(**inp):
    x = np.asarray(inp["x"], dtype=np.float32)
    out = np.empty((4, 8192, D), np.float32)
    cores = [(b, par) for b in range(4) for par in range(2)]
    a_in = _a_inputs(inp)
    if FUSED:
        maps = []
        for (b, par) in cores:
            d, amask = _common_inputs(inp, b, par, rel=True)
            d["x"] = np.concatenate([_shard_x(x, b, par), _shard_x(x, b, 1 - par)], axis=0)
            d.update(a_in)
            d.update(_b_inputs(inp, amask))
            maps.append(d)
        res = run_bass_kernel_spmd(_get_nc("AB"), maps, core_ids=list(range(8)))
        ys = [np.asarray(r["y"]) for r in res.results]
    else:
        maps = []
        for (b, par) in cores:
            d, amask = _common_inputs(inp, b, par)
            d["x"] = _shard_x(x, b, par)
            d.update(a_in)
            maps.append(d)
        resA = run_bass_kernel_spmd(_get_nc("A"), maps, core_ids=list(range(8))).results
        maps = []
        for ci, (b, par) in enumerate(cores):
            d, amask = _common_inputs(inp, b, par)
            d["x"] = np.asarray(resA[ci]["xB"])
            for nm in ("kT", "krT", "v"):
                d[nm + "_g"] = np.concatenate([np.asarray(resA[2 * b][nm + "_loc"]), np.asarray(resA[2 * b + 1][nm + "_loc"])], axis=0)
            d.update(_b_inputs(inp, amask))
            maps.append(d)
        res = run_bass_kernel_spmd(_get_nc("B"), maps, core_ids=list(range(8)))
        ys = [np.asarray(r["y"]) for r in res.results]
    for ci, (b, par) in enumerate(cores):
        out[b].reshape(64, 128, D)[par::2] = ys[ci].reshape(NBLK, 128, D)
    return out
```
